# Optimizing a Trainium2 kernel written in Bass

```python
import jax, jax.numpy as jnp
from jax import lax
import numpy as np

D_MODEL = 1024
BATCH = 2
SEQ = 8192
DEPTH = 4

CHUNK = 64
N_META = 16
HEAD_DIM = 64
H_FOX = 8
H_RWKV = 8
W_FOX = H_FOX * HEAD_DIM
W_RWKV = H_RWKV * HEAD_DIM
LORA_W = 64
LORA_A = 64
LORA_G = 128
N_RWKV_IN = 3 * W_RWKV + LORA_W + LORA_A + LORA_G
N_GATES = 2 * D_MODEL
N_IN = 3 * W_FOX + H_FOX + N_RWKV_IN + N_GATES
D_FF = 2816
CONV_W = 3
Q_BLOCK = 128
NORM_EPS = 1e-6
GN_EPS = HEAD_DIM * 1e-5

IN_SPLITS = (W_FOX, 2 * W_FOX, 3 * W_FOX, 3 * W_FOX + H_FOX, 3 * W_FOX + H_FOX + N_RWKV_IN)
RWKV_SPLITS = (W_RWKV, 2 * W_RWKV, 3 * W_RWKV, 3 * W_RWKV + LORA_W, 3 * W_RWKV + LORA_W + LORA_A)

kernel_name = 'hybrid_fox_rwkv7_streaming_block'


def rms_norm(x, g):
    xf = x.astype(jnp.float32)
    y = xf * lax.rsqrt(jnp.mean(xf * xf, axis=-1, keepdims=True) + NORM_EPS)
    return y.astype(x.dtype) * g


def fox_attention(q, k, v, fcum):
    b, h, length, n = q.shape
    nb = length // Q_BLOCK
    q_blocks = jnp.moveaxis(q.reshape(b, h, nb, Q_BLOCK, n), 2, 0)
    f_blocks = jnp.moveaxis(fcum.reshape(b, h, nb, Q_BLOCK), 2, 0)
    starts = jnp.arange(nb, dtype=jnp.int32) * Q_BLOCK
    key_pos = jnp.arange(length, dtype=jnp.int32)
    scale = HEAD_DIM ** -0.5

    def one_block(args):
        q_blk, f_blk, start = args
        s = jnp.einsum('bhqd,bhkd->bhqk', q_blk, k).astype(jnp.float32) * scale
        s = s + f_blk[..., :, None] - fcum[..., None, :]
        q_pos = start + jnp.arange(Q_BLOCK, dtype=jnp.int32)
        mask = key_pos[None, :] <= q_pos[:, None]
        s = jnp.where(mask, s, -jnp.inf)
        p = jax.nn.softmax(s, axis=-1)
        return jnp.einsum('bhqk,bhkd->bhqd', p.astype(v.dtype), v)

    out = lax.map(one_block, (q_blocks, f_blocks, starts))
    return jnp.moveaxis(out, 0, 2).reshape(b, h, length, n)


def rwkv7_scan(r, w, k, v, a, bb):
    b, length, h, n = r.shape

    def step(state, inp):
        r_t, w_t, k_t, v_t, a_t, b_t = inp
        sa = jnp.einsum('bhvk,bhk->bhv', state, a_t)
        state = (state * w_t[:, :, None, :]
                 + sa[..., None] * b_t[:, :, None, :]
                 + v_t[..., None] * k_t[:, :, None, :])
        return state, jnp.einsum('bhvk,bhk->bhv', state, r_t)

    xs = tuple(jnp.moveaxis(t.astype(jnp.float32), 1, 0) for t in (r, w, k, v, a, bb))
    s0 = jnp.zeros((b, h, n, n), jnp.float32)
    _, y = lax.scan(step, s0, xs)
    return jnp.moveaxis(y, 0, 1)


def hybrid_mixer(hn, w_in, q_g, k_g, f_b, mu, w0, w_up, a0, a_up, g_up, k_k, k_a, r_k,
                 gn_w, gn_b, w_bf, w_br, g_b, w_o):
    b, length, _ = hn.shape
    proj = hn @ w_in
    q, k, v, f_logit, rw, gates = jnp.split(proj, IN_SPLITS, axis=-1)

    def to_heads(t, nh):
        return t.reshape(b, length, nh, HEAD_DIM).transpose(0, 2, 1, 3)
    qh = rms_norm(to_heads(q, H_FOX), q_g)
    kh = rms_norm(to_heads(k, H_FOX), k_g)
    vh = to_heads(v, H_FOX)
    log_f = jax.nn.log_sigmoid((f_logit + f_b).astype(jnp.float32))
    fcum = jnp.cumsum(log_f, axis=1).transpose(0, 2, 1)
    o_fox = fox_attention(qh, kh, vh, fcum)
    o_fox = o_fox.transpose(0, 2, 1, 3).reshape(b, length, W_FOX)

    rw_prev = jnp.pad(rw, ((0, 0), (1, 0), (0, 0)))[:, :-1]
    z = rw + mu * (rw_prev - rw)
    r, kr, vr, zw, za, zg = jnp.split(z, RWKV_SPLITS, axis=-1)
    w_log = -jax.nn.softplus(-(w0 + jnp.tanh(zw) @ w_up)) - 0.5
    decay = jnp.exp(-jnp.exp(w_log.astype(jnp.float32)))
    a_rate = jax.nn.sigmoid(a0 + za @ a_up)
    g = jax.nn.sigmoid(zg) @ g_up
    kk = (kr * k_k).reshape(b, length, H_RWKV, HEAD_DIM).astype(jnp.float32)
    kk = kk / jnp.maximum(jnp.sqrt(jnp.sum(kk * kk, axis=-1, keepdims=True)), 1e-12)
    kr = kr * (1.0 + (a_rate - 1.0) * k_a)
    heads4 = lambda t: t.reshape(b, length, H_RWKV, HEAD_DIM)
    r4, k4, v4, a4 = heads4(r), heads4(kr), heads4(vr), heads4(a_rate)
    y = rwkv7_scan(r4, heads4(decay), k4, v4, -kk, kk * a4)
    mean = jnp.mean(y, axis=-1, keepdims=True)
    var = jnp.mean(jnp.square(y - mean), axis=-1, keepdims=True)
    yn = (y - mean) * lax.rsqrt(var + GN_EPS)
    yn = yn * gn_w.reshape(H_RWKV, HEAD_DIM) + gn_b.reshape(H_RWKV, HEAD_DIM)
    bonus = jnp.sum((r4 * k4 * r_k).astype(jnp.float32), axis=-1, keepdims=True) * v4
    o_rwkv = (yn + bonus).astype(hn.dtype).reshape(b, length, W_RWKV) * g

    g_fox, g_rwkv = jnp.split(jax.nn.sigmoid(gates + g_b), 2, axis=-1)
    merged = g_fox * (o_fox @ w_bf) + g_rwkv * (o_rwkv @ w_br)
    return merged @ w_o


def conv_ffn(hn, w_up, w_conv, w_down):
    u = hn @ w_up
    length = u.shape[1]
    u_pad = jnp.pad(u, ((0, 0), (CONV_W - 1, 0), (0, 0)))
    uc = sum(w_conv[j] * u_pad[:, j:j + length] for j in range(CONV_W))
    gate, val = jnp.split(uc, 2, axis=-1)
    return (jax.nn.silu(gate) * val) @ w_down


def setup_inputs(seed: int = 0) -> dict:
    key = jax.random.key(seed)
    ks = jax.random.split(key, 26)
    f32 = jnp.float32

    def nrm(k, shape, s):
        return s * jax.random.normal(k, shape, f32)

    return {
        'x': nrm(ks[0], (BATCH, SEQ, D_MODEL), 1.0),
        'meta_tokens': nrm(ks[1], (N_META, D_MODEL), 1.0),
        'norm_mix': 1.0 + nrm(ks[2], (DEPTH, D_MODEL), 0.02),
        'norm_ffn': 1.0 + nrm(ks[3], (DEPTH, D_MODEL), 0.02),
        'w_in': nrm(ks[4], (DEPTH, D_MODEL, N_IN), D_MODEL ** -0.5),
        'fox_q_norm': 1.0 + nrm(ks[5], (DEPTH, HEAD_DIM), 0.02),
        'fox_k_norm': 1.0 + nrm(ks[6], (DEPTH, HEAD_DIM), 0.02),
        'fox_f_bias': 2.0 + nrm(ks[7], (DEPTH, H_FOX), 1.0),
        'rwkv_shift_mu': jax.random.uniform(ks[8], (DEPTH, N_RWKV_IN), f32),
        'rwkv_w0': -3.0 + nrm(ks[9], (DEPTH, W_RWKV), 1.0),
        'rwkv_w_up': nrm(ks[10], (DEPTH, LORA_W, W_RWKV), 0.1 * LORA_W ** -0.5),
        'rwkv_a0': nrm(ks[11], (DEPTH, W_RWKV), 0.5),
        'rwkv_a_up': nrm(ks[12], (DEPTH, LORA_A, W_RWKV), 0.5 * LORA_A ** -0.5),
        'rwkv_g_up': nrm(ks[13], (DEPTH, LORA_G, W_RWKV), LORA_G ** -0.5),
        'rwkv_k_k': 0.85 + nrm(ks[14], (DEPTH, W_RWKV), 0.1),
        'rwkv_k_a': 1.0 + nrm(ks[15], (DEPTH, W_RWKV), 0.1),
        'rwkv_r_k': nrm(ks[16], (DEPTH, H_RWKV, HEAD_DIM), 0.1),
        'rwkv_gn_w': 1.0 + nrm(ks[17], (DEPTH, W_RWKV), 0.02),
        'rwkv_gn_b': nrm(ks[18], (DEPTH, W_RWKV), 0.02),
        'w_branch_fox': nrm(ks[19], (DEPTH, W_FOX, D_MODEL), W_FOX ** -0.5),
        'w_branch_rwkv': nrm(ks[20], (DEPTH, W_RWKV, D_MODEL), W_RWKV ** -0.5),
        'gate_bias': nrm(ks[21], (DEPTH, N_GATES), 0.1),
        'w_out': nrm(ks[22], (DEPTH, D_MODEL, D_MODEL), 0.5 * D_MODEL ** -0.5),
        'ffn_up': nrm(ks[23], (DEPTH, D_MODEL, 2 * D_FF), D_MODEL ** -0.5),
        'ffn_conv': nrm(ks[24], (DEPTH, CONV_W, 2 * D_FF), CONV_W ** -0.5),
        'ffn_down': nrm(ks[25], (DEPTH, D_FF, D_MODEL), 0.5 * D_FF ** -0.5),
    }


def reference(x, meta_tokens, norm_mix, norm_ffn, w_in, fox_q_norm, fox_k_norm, fox_f_bias,
              rwkv_shift_mu, rwkv_w0, rwkv_w_up, rwkv_a0, rwkv_a_up, rwkv_g_up, rwkv_k_k,
              rwkv_k_a, rwkv_r_k, rwkv_gn_w, rwkv_gn_b, w_branch_fox, w_branch_rwkv,
              gate_bias, w_out, ffn_up, ffn_conv, ffn_down):
    b, s, _ = x.shape
    meta = jnp.broadcast_to(meta_tokens.astype(x.dtype)[None], (b, N_META, D_MODEL))
    h = jnp.concatenate([meta, x], axis=1)
    length = N_META + s
    padded = -(-length // Q_BLOCK) * Q_BLOCK
    h = jnp.pad(h, ((0, 0), (0, padded - length), (0, 0)))
    for i in range(DEPTH):
        h = h + hybrid_mixer(rms_norm(h, norm_mix[i]), w_in[i], fox_q_norm[i], fox_k_norm[i],
                             fox_f_bias[i], rwkv_shift_mu[i], rwkv_w0[i], rwkv_w_up[i],
                             rwkv_a0[i], rwkv_a_up[i], rwkv_g_up[i], rwkv_k_k[i], rwkv_k_a[i],
                             rwkv_r_k[i], rwkv_gn_w[i], rwkv_gn_b[i], w_branch_fox[i],
                             w_branch_rwkv[i], gate_bias[i], w_out[i])
        h = h + conv_ffn(rms_norm(h, norm_ffn[i]), ffn_up[i], ffn_conv[i], ffn_down[i])
    return h[:, N_META:N_META + s]
```

```python
import numpy as np
from contextlib import ExitStack
import concourse.bass as bass
import concourse.mybir as mybir
from concourse.bass_utils import run_bass_kernel_spmd

F32 = mybir.dt.float32
BF16 = mybir.dt.bfloat16
AF = mybir.ActivationFunctionType
ALU = mybir.AluOpType
AX = mybir.AxisListType

ENGS = ("pe", "act", "dve", "pool", "sp")


class Buf:
    __slots__ = ("name", "w", "r", "sem", "dcount")

    def __init__(self, name):
        self.name = name
        self.w = None
        self.r = {}
        self.sem = None
        self.dcount = 0


class Op:
    __slots__ = ("eng", "fn", "deps", "dma_owner", "dma_val", "signal", "sigval", "idx")

    def __init__(self, eng, fn):
        self.eng = eng
        self.fn = fn
        self.deps = []
        self.dma_owner = None
        self.dma_val = 0
        self.signal = False
        self.sigval = 0


class Prog:
    def __init__(self, nc):
        self.nc = nc
        self.ops = {e: [] for e in ENGS}
        self.stack = ExitStack()
        self.owners = []
        self.nbuf = 0

    def sbuf(self, name, shape, dt):
        return self.stack.enter_context(self.nc.sbuf_tensor(name, list(shape), dt))

    def psum(self, name, shape, dt=F32):
        return self.stack.enter_context(self.nc.psum_tensor(name, list(shape), dt))

    def buf(self, name=None):
        self.nbuf += 1
        return Buf(name or f"b{self.nbuf}")

    def _record(self, o, reads, writes):
        deps = {}
        for b in reads:
            if b.w is not None:
                deps[id(b.w)] = b.w
        for b in writes:
            if b.w is not None and (b.w.eng != o.eng or b.w.dma_owner is not None or o.dma_owner is not None):
                deps[id(b.w)] = b.w
            for rd in b.r.values():
                if rd.eng != o.eng or rd.dma_owner is not None or o.dma_owner is not None:
                    deps[id(rd)] = rd
        o.deps = list(deps.values())
        for d in o.deps:
            if d.dma_owner is None:
                d.signal = True
        for b in reads:
            key = o.eng if o.dma_owner is None else ("dma", id(o.dma_owner))
            b.r[key] = o
        for b in writes:
            b.w = o
            b.r = {}
        self.ops[o.eng].append(o)
        return o

    def op(self, eng, fn, reads=(), writes=()):
        return self._record(Op(eng, fn), reads, writes)

    def dma(self, eng, out, in_, owner, reads=(), writes=(), **kw):
        o = Op(eng, lambda e: e.dma_start(out=out, in_=in_, **kw))
        o.dma_owner = owner
        owner.dcount += 1
        o.dma_val = 16 * owner.dcount
        if owner.sem is None:
            owner.sem = True
            self.owners.append(owner)
        return self._record(o, reads, writes)

    def emit(self):
        nc = self.nc
        st = self.stack
        esem = {e: st.enter_context(nc.semaphore(f"s_{e}")) for e in ENGS if e != "sp"}
        for i, b in enumerate(self.owners):
            b.sem = st.enter_context(nc.semaphore(f"d{i}"))
        for e in ENGS:
            c = 0
            for o in self.ops[e]:
                if o.signal:
                    c += 1
                    o.sigval = c
        final = {e: 0 for e in ENGS}
        for e in ENGS:
            if e == "sp":
                continue
            if self.ops[e]:
                last = [o for o in self.ops[e] if o.dma_owner is None]
                if last:
                    lo = last[-1]
                    if not lo.signal:
                        lo.signal = True
                        c = 0
                        for o in self.ops[e]:
                            if o.signal:
                                c += 1
                                o.sigval = c
                    final[e] = lo.sigval
        block = st.enter_context(nc.Block())

        def run(e, eng):
            seen = {}
            for o in self.ops[e]:
                for d in o.deps:
                    if d.dma_owner is not None:
                        sem, val = d.dma_owner.sem, d.dma_val
                    else:
                        sem, val = esem[d.eng], d.sigval
                    k = id(sem)
                    if seen.get(k, 0) >= val:
                        continue
                    seen[k] = val
                    eng.wait_ge(sem, val)
                ins = o.fn(eng)
                if o.dma_owner is not None:
                    ins.then_inc(o.dma_owner.sem, 16)
                elif o.signal:
                    ins.then_inc(esem[e], 1)
            if e == "sp":
                for ee in ENGS:
                    if ee != "sp" and final[ee] > 0:
                        eng.wait_ge(esem[ee], final[ee])
                for b in self.owners:
                    eng.wait_ge(b.sem, 16 * b.dcount)

        @block.tensor
        def _(eng):
            run("pe", eng)

        @block.scalar
        def _(eng):
            run("act", eng)

        @block.vector
        def _(eng):
            run("dve", eng)

        @block.gpsimd
        def _(eng):
            run("pool", eng)

        @block.sync
        def _(eng):
            run("sp", eng)

    def close(self):
        self.stack.close()


class KB:
    def __init__(self, nc, n_ps=8):
        self.nc = nc
        self.P = Prog(nc)
        self.banks = [(self.P.psum(f"psb{i}", [128, 512]), self.P.buf(f"psb{i}")) for i in range(n_ps)]
        self.bi = 0
        self.pools = {}
        self.ev = 0

    def ps(self):
        t = self.banks[self.bi % len(self.banks)]
        self.bi += 1
        return t

    def tmp_ps(self, key, n=2):
        k = ("ps", key)
        if k not in self.pools:
            self.pools[k] = [[(self.P.psum(f"{key}_{i}", [128, 512]), self.P.buf(f"{key}_{i}")) for i in range(n)], 0]
        pl = self.pools[k]
        t = pl[0][pl[1] % n]
        pl[1] += 1
        return t

    def tile(self, name, shape, dt=F32):
        return self.P.sbuf("sb_" + name, shape, dt), self.P.buf(name)

    def tmp(self, key, shape, dt=F32, n=2):
        k = (key, tuple(shape), str(dt))
        if k not in self.pools:
            self.pools[k] = [[self.tile(f"{key}_{i}", shape, dt) for i in range(n)], 0]
        pl = self.pools[k]
        t = pl[0][pl[1] % n]
        pl[1] += 1
        return t

    def dram_in(self, name, shape, dt=F32):
        return self.nc.dram_tensor(name, list(shape), dt, kind="ExternalInput").ap()

    def dram_out(self, name, shape, dt=F32):
        return self.nc.dram_tensor(name, list(shape), dt, kind="ExternalOutput").ap()

    def load(self, out, in_, owner, eng="sp", reads=()):
        self.P.dma(eng, out, in_, owner, reads=reads, writes=[owner])

    def store(self, out, in_, owner, eng="sp", writes=()):
        self.P.dma(eng, out, in_, owner, reads=[owner], writes=writes)

    def mm(self, out, lhsT, rhs, start, stop, r, w):
        self.P.op("pe", lambda e: e.matmul(out, lhsT, rhs, start=start, stop=stop), r, w)

    def act(self, out, in_, func, r, w, bias=None, scale=None, accum=None, eng="act"):
        kw = {}
        if bias is not None:
            kw["bias"] = bias
        if scale is not None:
            kw["scale"] = scale
        if accum is not None:
            kw["accum_out"] = accum
        self.P.op(eng, lambda e: e.activation(out=out, in_=in_, func=func, **kw), r, w)

    def tt(self, out, in0, in1, op, r, w, eng="dve"):
        self.P.op(eng, lambda e: e.tensor_tensor(out=out, in0=in0, in1=in1, op=op), r, w)

    def ts(self, out, in0, s1, s2, op0, op1, r, w, eng="dve"):
        if op1 is None:
            self.P.op(eng, lambda e: e.tensor_scalar(out=out, in0=in0, scalar1=s1, scalar2=None, op0=op0), r, w)
        else:
            self.P.op(eng, lambda e: e.tensor_scalar(out=out, in0=in0, scalar1=s1, scalar2=s2, op0=op0, op1=op1), r, w)

    def stt(self, out, in0, scalar, in1, op0, op1, r, w, eng="dve"):
        self.P.op(eng, lambda e: e.scalar_tensor_tensor(out=out, in0=in0, scalar=scalar, in1=in1, op0=op0, op1=op1), r, w)

    def copy(self, out, in_, r, w, eng=None):
        if eng is None:
            self.ev += 1
            eng = "act" if self.ev % 2 else "dve"
        if eng == "act":
            self.P.op("act", lambda e: e.activation(out=out, in_=in_, func=AF.Copy), r, w)
        else:
            self.P.op(eng, lambda e: e.tensor_copy(out=out, in_=in_), r, w)

    def recip(self, out, in_, r, w):
        self.P.op("dve", lambda e: e.reciprocal(out=out, in_=in_), r, w)

    def memset(self, out, val, w, eng="pool"):
        self.P.op(eng, lambda e: e.memset(out, val), (), w)

    def scan(self, out, d0, d1, init, op0, op1, r, w):
        self.P.op("dve", lambda e: e.tensor_tensor_scan(out=out, data0=d0, data1=d1, initial=init, op0=op0, op1=op1), r, w)

    def finish(self):
        self.P.emit()
        self.P.close()
        return self.nc


TOK = 2064
NTL = [(0, 16)] + [(16 + 512 * i, 512) for i in range(4)]
TBL = [(0, 16)] + [(16 + 128 * i, 128) for i in range(16)]
E05 = 0.6065306597126334
NORM_EPS = 1e-6
RW0 = 1544
GT0 = 3336


def emit_A(K, hT, w_in, gmix, gbias, mu, pr, w_up, a_up, g_up, muv, cst,
           qkT, vtok, fT, rw5, vr, bs, gtok, gatesT):
    hn, b_hn = K.tile("hn", [128, 8, TOK + 1], BF16)
    K.memset(hn[:, :, 0:1], 0.0, [b_hn])
    cs, b_c = K.tile("cst", [128, 258])
    K.load(cs[:], cst[:, :], b_c)
    ones_f, ones_bd, ind2 = cs[:, 0:128], cs[:, 128:256], cs[:, 256:258]
    gm, b_gm = K.tile("gm", [128, 8]); K.load(gm[:], gmix[:, :], b_gm)
    gb, b_gb = K.tile("gb", [128, 16]); K.load(gb[:], gbias[:, :], b_gb)
    mus, b_mu = K.tile("mus", [128, 14]); K.load(mus[:], mu[:, :], b_mu)
    prs, b_pr = K.tile("prs", [128, 5, 4]); K.load(prs[:], pr[:, :, :], b_pr)
    omka, b_omka = K.tile("omka", [128, 4])
    K.ts(omka[:], prs[:, 3, :], -1.0, 1.0, ALU.mult, ALU.add, [b_pr], [b_omka])
    wup, b_wup = K.tile("wup", [64, 512]); K.load(wup[:], w_up[:, :], b_wup)
    aup, b_aup = K.tile("aup", [128, 512]); K.load(aup[64:128, :], a_up[:, :], b_aup)
    gup, b_gup = K.tile("gup", [128, 512]); K.load(gup[:], g_up[:, :], b_gup)
    mub, b_mub = K.tile("mub", [128, 512]); K.load(mub[:], muv[0:1, :].partition_broadcast(128), b_mub)
    omm, b_omm = K.tile("omm", [128, 512])
    K.ts(omm[:], mub[:], -1.0, 1.0, ALU.mult, ALU.add, [b_mub], [b_omm])

    hv = hT.rearrange("(kc p) t -> p kc t", p=128)
    for (n0, nn) in NTL:
        hs, b_hs = K.tmp("hs", [128, 8, 512], F32, 1)
        K.load(hs[:, :, :nn], hv[:, :, n0:n0 + nn], b_hs)
        ps, b_ps = K.ps()
        for kc in range(8):
            sq, b_sq = K.tmp("sq", [128, 512], F32, 2)
            K.act(sq[:, :nn], hs[:, kc, :nn], AF.Square, [b_hs], [b_sq])
            K.mm(ps[:, :nn], ones_f, sq[:, :nn], kc == 0, kc == 7, [b_sq, b_c], [b_ps])
        rs, b_rs = K.tmp("rs", [128, 512], F32, 2)
        K.act(rs[:, :nn], ps[:, :nn], AF.Sqrt, [b_ps], [b_rs], bias=NORM_EPS, scale=1.0 / 1024)
        K.recip(rs[:, :nn], rs[:, :nn], [b_rs], [b_rs])
        for kc in range(8):
            K.stt(hn[:, kc, 1 + n0:1 + n0 + nn], hs[:, kc, :nn], gm[:, kc:kc + 1], rs[:, :nn],
                  ALU.mult, ALU.mult, [b_hs, b_gm, b_rs], [b_hn])

    wv = w_in.rearrange("(kc p) c -> p kc c", p=128)

    def load_w(c0, ncol):
        wt, b_wt = K.tmp("wt", [128, 8, 512], BF16, 3)
        K.load(wt[:, :, :ncol], wv[:, :, c0:c0 + ncol], b_wt, eng="pool")
        return wt, b_wt

    def fm_tile(wt, b_wt, c0, m, evac):
        for (n0, nn) in NTL:
            ps, b_ps = K.ps()
            for kc in range(8):
                K.mm(ps[:m, :nn], wt[:, kc, c0:c0 + m], hn[:, kc, 1 + n0:1 + n0 + nn], kc == 0, kc == 7,
                     [b_wt, b_hn], [b_ps])
            evac(ps, b_ps, n0, nn)

    def ev_store(dst, m, func=None, bias=None, rd=()):
        def f(ps, b_ps, n0, nn):
            st, b_st = K.tmp("st", [128, 512], F32, 3)
            if func is None:
                K.copy(st[:m, :nn], ps[:m, :nn], [b_ps], [b_st])
            else:
                K.act(st[:m, :nn], ps[:m, :nn], func, [b_ps] + list(rd), [b_st], bias=bias)
            K.store(dst[:, n0:n0 + nn], st[:m, :nn], b_st)
        return f

    groups = [("q", 0, 512), ("k", 512, 512), ("v", 1024, 512), ("f", 1536, 8),
              ("lora", RW0 + 1536, 256), ("rv", RW0 + 1024, 512), ("r", RW0, 512), ("k2", RW0 + 512, 512)] + \
             [("g", GT0 + 512 * i, 512) for i in range(4)]
    loaded = {}

    def get_w(i):
        if i < len(groups) and i not in loaded:
            loaded[i] = load_w(groups[i][1], groups[i][2])
        return loaded.get(i)

    z12, b_z12 = K.tile("z12", [128, TOK])
    psbs = K.P.psum("psbs", [128, 512]); b_psbs = K.P.buf("psbs")

    def rw_tile(wt, b_wt, c0, mucol, zout, b_zout):
        praw, b_praw = K.tmp("praw", [128, TOK + 1], F32, 1)
        K.memset(praw[:, 0:1], 0.0, [b_praw])
        fm_tile(wt, b_wt, c0, 128,
                lambda ps, b_ps, n0, nn: K.copy(praw[:, 1 + n0:1 + n0 + nn], ps[:, :nn], [b_ps], [b_praw]))
        for (n0, nn) in NTL:
            d, b_d = K.tmp("d", [128, 512], F32, 2)
            K.tt(d[:, :nn], praw[:, n0:n0 + nn], praw[:, 1 + n0:1 + n0 + nn], ALU.subtract, [b_praw], [b_d])
            K.stt(zout[:, n0:n0 + nn], d[:, :nn], mus[:, mucol:mucol + 1], praw[:, 1 + n0:1 + n0 + nn],
                  ALU.mult, ALU.add, [b_d, b_mu, b_praw], [b_zout])

    get_w(0)
    for gi, (kind, c0g, ncol) in enumerate(groups):
        wt, b_wt = get_w(gi)
        get_w(gi + 1)
        if kind in ("q", "k"):
            for i in range(4):
                row0 = c0g + 128 * i
                fm_tile(wt, b_wt, 128 * i, 128, ev_store(qkT[row0:row0 + 128, :], 128))
        elif kind == "f":
            fm_tile(wt, b_wt, 0, 8, ev_store(fT[0:8, :], 8))
        elif kind == "g":
            for i in range(4):
                gi_t = (c0g - GT0) // 128 + i
                fm_tile(wt, b_wt, 128 * i, 128,
                        ev_store(gatesT[gi_t * 128:(gi_t + 1) * 128, :], 128, AF.Sigmoid, gb[:, gi_t:gi_t + 1], [b_gb]))
        elif kind == "v":
            for (t0, nt) in TBL:
                ps, b_ps = K.ps()
                for kc in range(8):
                    K.mm(ps[:nt, :512], hn[:, kc, 1 + t0:1 + t0 + nt], wt[:, kc, 0:512], kc == 0, kc == 7,
                         [b_hn, b_wt], [b_ps])
                sv, b_sv = K.tmp("sv", [128, 512], BF16, 3)
                K.copy(sv[:nt, :], ps[:nt, :512], [b_ps], [b_sv])
                K.store(vtok[t0:t0 + nt, :], sv[:nt, :], b_sv)
        elif kind == "rv":
            for (t0, nt) in TBL:
                ps1, b_ps1 = K.ps()
                ps2, b_ps2 = K.ps()
                for kc in range(8):
                    K.mm(ps1[:nt, :512], hn[:, kc, 1 + t0:1 + t0 + nt], wt[:, kc, 0:512], kc == 0, kc == 7,
                         [b_hn, b_wt], [b_ps1])
                for kc in range(8):
                    K.mm(ps2[:nt, :512], hn[:, kc, t0:t0 + nt], wt[:, kc, 0:512], kc == 0, kc == 7,
                         [b_hn, b_wt], [b_ps2])
                t1, b_t1 = K.tmp("rvt1", [128, 512], F32, 1)
                t2, b_t2 = K.tmp("rvt2", [128, 512], F32, 1)
                K.tt(t1[:nt, :], ps1[:nt, :512], omm[:nt, :], ALU.mult, [b_ps1, b_omm], [b_t1])
                K.tt(t2[:nt, :], ps2[:nt, :512], mub[:nt, :], ALU.mult, [b_ps2, b_mub], [b_t2])
                st, b_st = K.tmp("st", [128, 512], F32, 3)
                K.tt(st[:nt, :], t1[:nt, :], t2[:nt, :], ALU.add, [b_t1, b_t2], [b_st], eng="pool")
                K.store(vr[t0:t0 + nt, :], st[:nt, :], b_st)
        elif kind == "lora":
            rw_tile(wt, b_wt, 0, 12, z12[:], b_z12)
            K.act(z12[0:64, :], z12[0:64, :], AF.Tanh, [b_z12], [b_z12])
            sg, b_sg = K.tmp("zbig", [128, TOK], F32, 3)
            rw_tile(wt, b_wt, 128, 13, sg[:], b_sg)
            K.act(sg[:], sg[:], AF.Sigmoid, [b_sg], [b_sg])
            for (t0, nt) in TBL:
                ps, b_ps = K.ps()
                K.mm(ps[:nt, :512], sg[:, t0:t0 + nt], gup[:], True, True, [b_sg, b_gup], [b_ps])
                st, b_st = K.tmp("st", [128, 512], F32, 3)
                K.copy(st[:nt, :], ps[:nt, :512], [b_ps], [b_st])
                K.store(gtok[t0:t0 + nt, :], st[:nt, :], b_st)
        elif kind == "k2":
            pass
        elif kind == "r":
            wtk, b_wtk = get_w(gi + 1)
            for hp in range(4):
                hs_ = slice(hp * 128, (hp + 1) * 128)
                zr, b_zr = K.tmp("zbig", [128, TOK], F32, 3)
                rw_tile(wt, b_wt, 128 * hp, hp, zr[:], b_zr)
                K.store(rw5[0, hs_, :], zr[:], b_zr)
                zk, b_zk = K.tmp("zbig", [128, TOK], F32, 3)
                rw_tile(wtk, b_wtk, 128 * hp, 4 + hp, zk[:], b_zk)
                if hp == 3:
                    get_w(gi + 2)
                for (n0, nn) in NTL:
                    ns = slice(n0, n0 + nn)
                    psW, b_psW = K.ps()
                    K.mm(psW[:, :nn], wup[0:64, hs_], z12[0:64, ns], True, True, [b_wup, b_z12], [b_psW])
                    lsig, b_lsig = K.tmp("lsig", [128, 512], F32, 2)
                    K.act(lsig[:, :nn], psW[:, :nn], AF.Sigmoid, [b_psW, b_pr], [b_lsig], bias=prs[:, 0, hp:hp + 1])
                    K.store(rw5[4, hs_, ns], lsig[:, :nn], b_lsig)
                    psA, b_psA = K.ps()
                    K.mm(psA[:, :nn], aup[64:128, hs_], z12[64:128, ns], True, True, [b_aup, b_z12], [b_psA])
                    ar, b_ar = K.tmp("ar", [128, 512], F32, 1)
                    K.act(ar[:, :nn], psA[:, :nn], AF.Sigmoid, [b_psA, b_pr], [b_ar], bias=prs[:, 1, hp:hp + 1])
                    kk, b_kk = K.tmp("kk", [128, 512], F32, 1)
                    K.ts(kk[:, :nn], zk[:, ns], prs[:, 2, hp:hp + 1], None, ALU.mult, None, [b_zk, b_pr], [b_kk])
                    sqk, b_sqk = K.tmp("sqk", [128, 512], F32, 1)
                    K.act(sqk[:, :nn], kk[:, :nn], AF.Square, [b_kk], [b_sqk])
                    psS, b_psS = K.ps()
                    K.mm(psS[:, :nn], ones_bd, sqk[:, :nn], True, True, [b_c, b_sqk], [b_psS])
                    rn, b_rn = K.tmp("rn", [128, 512], F32, 1)
                    K.act(rn[:, :nn], psS[:, :nn], AF.Sqrt, [b_psS], [b_rn])
                    K.ts(rn[:, :nn], rn[:, :nn], 1e-12, None, ALU.max, None, [b_rn], [b_rn])
                    K.recip(rn[:, :nn], rn[:, :nn], [b_rn], [b_rn])
                    kkn, b_kkn = K.tmp("kkn", [128, 512], F32, 2)
                    K.tt(kkn[:, :nn], kk[:, :nn], rn[:, :nn], ALU.mult, [b_kk, b_rn], [b_kkn])
                    K.store(rw5[2, hs_, ns], kkn[:, :nn], b_kkn)
                    bb, b_bb = K.tmp("bb", [128, 512], F32, 2)
                    K.tt(bb[:, :nn], kkn[:, :nn], ar[:, :nn], ALU.mult, [b_kkn, b_ar], [b_bb], eng="pool")
                    K.store(rw5[3, hs_, ns], bb[:, :nn], b_bb)
                    tq, b_tq = K.tmp("tq", [128, 512], F32, 1)
                    K.ts(tq[:, :nn], ar[:, :nn], prs[:, 3, hp:hp + 1], omka[:, hp:hp + 1], ALU.mult, ALU.add,
                         [b_ar, b_pr, b_omka], [b_tq])
                    km, b_km = K.tmp("km", [128, 512], F32, 2)
                    K.tt(km[:, :nn], zk[:, ns], tq[:, :nn], ALU.mult, [b_zk, b_tq], [b_km])
                    K.store(rw5[1, hs_, ns], km[:, :nn], b_km)
                    rk, b_rk = K.tmp("rk", [128, 512], F32, 1)
                    K.stt(rk[:, :nn], zr[:, ns], prs[:, 4, hp:hp + 1], km[:, :nn], ALU.mult, ALU.mult,
                          [b_zr, b_pr, b_km], [b_rk])
                    for tb, (t0, nt) in enumerate(TBL):
                        if n0 <= t0 < n0 + nn:
                            K.mm(psbs[:nt, 8 * tb + 2 * hp:8 * tb + 2 * hp + 2], rk[:, t0 - n0:t0 - n0 + nt], ind2,
                                 True, True, [b_rk, b_c], [b_psbs])
    bst, b_bst = K.tile("bst", [128, 136])
    K.copy(bst[:], psbs[:, 0:136], [b_psbs], [b_bst], eng="dve")
    for tb, (t0, nt) in enumerate(TBL):
        K.store(bs[t0:t0 + nt, :], bst[:nt, 8 * tb:8 * tb + 8], b_bst)


LP = 8320
NPAD = 112
GN_EPS = 64 * 1e-5


def emit_H(K, qT2, kT2, v2, f2, rw5h, vrh, hpar, hcst, ofT, yout):
    cs, b_c = K.tile("hc", [128, 768])
    K.load(cs[:], hcst[:, :], b_c)
    ident, bdm, m_su, m_sl, m_u = cs[:, 0:128], cs[:, 128:256], cs[:, 256:384], cs[:, 384:512], cs[:, 512:640]
    m_full = cs[:, 640:768]
    hp_, b_hp = K.tile("hpar", [128, 4]); K.load(hp_[:], hpar[:, :], b_hp)

    SEG = 1024
    segs = [(s0, min(SEG, LP - s0)) for s0 in range(0, LP, SEG)]
    rmask, b_rmask = K.tile("rmask", [128, SEG])
    K.memset(rmask[:], 1.0, [b_rmask])
    K.memset(rmask[:].rearrange("p (c s) -> p c s", s=64)[:, :, 0:1], 0.0, [b_rmask])
    ST, b_ST = K.tmp("ST", [128, 64], F32, 2)
    K.memset(ST[:], 0.0, [b_ST])
    for (s0, sl) in segs:
        nch = sl // 64
        xin = []
        for i in range(5):
            t, b = K.tmp(f"rin{i}", [128, SEG], F32, 2)
            K.load(t[:, :sl], rw5h[i, :, s0:s0 + sl], b)
            xin.append((t, b))
        (r_, b_r), (km_, b_km), (kn_, b_kn), (bb_, b_bb), (ls_, b_ls) = xin
        vseg, b_vseg = K.tmp("vseg", [128, 16, 64], F32, 2)
        for h in range(2):
            K.load(vseg[h * 64:(h + 1) * 64, :nch, :],
                   vrh[s0:s0 + sl, h * 64:(h + 1) * 64].rearrange("(c s) v -> s c v", s=64), b_vseg)
        cl, b_cl = K.tmp("cl", [128, SEG], F32, 1)
        K.scan(cl[:, :sl], rmask[:, :sl], ls_[:, :sl], 0.0, ALU.mult, ALU.add, [b_rmask, b_ls], [b_cl])
        Wt, b_Wt = K.tmp("Wt", [128, SEG], F32, 1)
        Wi, b_Wi = K.tmp("Wi", [128, SEG], F32, 1)
        Wp, b_Wp = K.tmp("Wp", [128, SEG], F32, 1)
        K.act(Wt[:, :sl], cl[:, :sl], AF.Exp, [b_cl], [b_Wt], scale=-E05)
        K.act(Wi[:, :sl], cl[:, :sl], AF.Exp, [b_cl], [b_Wi], scale=E05)
        K.tt(Wp[:, :sl], cl[:, :sl], ls_[:, :sl], ALU.subtract, [b_cl, b_ls], [b_Wp], eng="pool")
        K.act(Wp[:, :sl], Wp[:, :sl], AF.Exp, [b_Wp], [b_Wp], scale=-E05)
        K.tt(r_[:, :sl], r_[:, :sl], Wt[:, :sl], ALU.mult, [b_r, b_Wt], [b_r])
        K.tt(km_[:, :sl], km_[:, :sl], Wi[:, :sl], ALU.mult, [b_km, b_Wi], [b_km], eng="pool")
        K.tt(bb_[:, :sl], bb_[:, :sl], Wi[:, :sl], ALU.mult, [b_bb, b_Wi], [b_bb])
        K.stt(kn_[:, :sl], kn_[:, :sl], -1.0, Wp[:, :sl], ALU.mult, ALU.mult, [b_kn, b_Wp], [b_kn])
        yseg, b_yseg = K.tmp("yseg", [128, 16, 64], F32, 2)
        for c in range(nch):
            csl = slice(c * 64, (c + 1) * 64)

            def bd(src, b_src, key, eng):
                t, b = K.tmp(key, [128, 128], F32, 2)
                K.tt(t[:].rearrange("p (a s) -> p a s", a=2), src[:, csl].unsqueeze(1).to_broadcast([128, 2, 64]),
                     bdm.rearrange("p (a s) -> p a s", a=2), ALU.mult, [b_src, b_c], [b], eng=eng)
                return t, b
            at, b_at = bd(kn_, b_kn, "at_bd", "dve")
            bt, b_bt = bd(bb_, b_bb, "bt_bd", "pool")
            kt, b_kt = bd(km_, b_km, "kt_bd", "pool")
            rt, b_rt = bd(r_, b_r, "rt_bd", "dve")

            def mmask(lh, b_lh, rh, b_rh, mask, key):
                ps, b_ps = K.ps()
                K.mm(ps[:, 0:128], lh[:], rh[:], True, True, [b_lh, b_rh], [b_ps])
                t, b = K.tmp(key, [128, 128], F32, 2)
                K.tt(t[:], ps[:, 0:128], mask, ALU.mult, [b_ps, b_c], [b])
                return t, b
            N, b_N = mmask(bt, b_bt, at, b_at, m_su, "N")
            Nt, b_Nt = mmask(at, b_at, bt, b_bt, m_sl, "Nt")
            AkT, b_AkT = mmask(kt, b_kt, at, b_at, m_su, "AkT")
            RbT, b_RbT = mmask(bt, b_bt, rt, b_rt, m_u, "RbT")
            RkT, b_RkT = mmask(kt, b_kt, rt, b_rt, m_u, "RkT")
            def tr(src, b_src, key):
                ps, b_ps = K.ps()
                K.mm(ps[:, 0:128], src[:], ident, True, True, [b_src, b_c], [b_ps])
                t, b = K.tmp(key, [128, 128], F32, 2)
                K.copy(t[:], ps[:, 0:128], [b_ps], [b])
                return t, b
            btT, b_btT = tr(bt, b_bt, "btT")
            ktT, b_ktT = tr(kt, b_kt, "ktT")
            Pm, b_P = K.tmp("Pinv", [128, 128], F32, 3)
            K.tt(Pm[:], N[:], ident, ALU.add, [b_N, b_c], [b_P], eng="pool")
            M, b_M, Mt, b_Mt = N, b_N, Nt, b_Nt
            for lev in range(5):
                ps2, b_ps2 = K.ps()
                K.mm(ps2[:, 0:128], M[:], Mt[:], True, True, [b_M, b_Mt], [b_ps2])
                Mt2, b_Mt2 = K.tmp("Mt2", [128, 128], F32, 3)
                K.copy(Mt2[:], ps2[:, 0:128], [b_ps2], [b_Mt2])
                if lev < 4:
                    ps1, b_ps1 = K.ps()
                    K.mm(ps1[:, 0:128], Mt[:], M[:], True, True, [b_M, b_Mt], [b_ps1])
                    M2, b_M2 = K.tmp("M2", [128, 128], F32, 3)
                    K.copy(M2[:], ps1[:, 0:128], [b_ps1], [b_M2])
                ps3, b_ps3 = K.ps()
                K.mm(ps3[:, 0:128], Mt2[:], Pm[:], True, True, [b_Mt2, b_P], [b_ps3])
                Pn, b_Pn = K.tmp("Pinv", [128, 128], F32, 3)
                K.tt(Pn[:], ps3[:, 0:128], Pm[:], ALU.add, [b_ps3, b_P], [b_Pn])
                Pm, b_P = Pn, b_Pn
                if lev < 4:
                    M, b_M, Mt, b_Mt = M2, b_M2, Mt2, b_Mt2
            Vc = vseg[:, c, :]
            psx, b_psx = K.ps()
            K.mm(psx[:, 0:64], at[:], ST[:], True, False, [b_at, b_ST], [b_psx])
            K.mm(psx[:, 0:64], AkT[:], Vc, False, True, [b_AkT, b_vseg], [b_psx])
            Xs, b_Xs = K.tmp("Xs", [128, 64], F32, 2)
            K.copy(Xs[:], psx[:, 0:64], [b_psx], [b_Xs], eng="act")
            psu, b_psu = K.ps()
            K.mm(psu[:, 0:64], Pm[:], Xs[:], True, True, [b_P, b_Xs], [b_psu])
            Us, b_Us = K.tmp("Us", [128, 64], F32, 2)
            K.copy(Us[:], psu[:, 0:64], [b_psu], [b_Us], eng="dve")
            psy, b_psy = K.ps()
            K.mm(psy[:, 0:64], rt[:], ST[:], True, False, [b_rt, b_ST], [b_psy])
            K.mm(psy[:, 0:64], RbT[:], Us[:], False, False, [b_RbT, b_Us], [b_psy])
            K.mm(psy[:, 0:64], RkT[:], Vc, False, True, [b_RkT, b_vseg], [b_psy])
            K.copy(yseg[:, c, :], psy[:, 0:64], [b_psy], [b_yseg], eng="act")
            pss, b_pss = K.ps()
            K.mm(pss[:, 0:64], ident, ST[:], True, False, [b_c, b_ST], [b_pss])
            K.mm(pss[:, 0:64], btT[:], Us[:], False, False, [b_btT, b_Us], [b_pss])
            K.mm(pss[:, 0:64], ktT[:], Vc, False, True, [b_ktT, b_vseg], [b_pss])
            STn, b_STn = K.tmp("ST", [128, 64], F32, 2)
            K.ts(STn[:], pss[:, 0:64], Wt[:, c * 64 + 63:c * 64 + 64], None, ALU.mult, None, [b_pss, b_Wt], [b_STn])
            ST, b_ST = STn, b_STn
        for h in range(2):
            K.store(yout[s0:s0 + sl, h * 64:(h + 1) * 64].rearrange("(c s) v -> s c v", s=64),
                    yseg[h * 64:(h + 1) * 64, :nch, :], b_yseg)

    identb, b_identb = K.tile("identb", [128, 128], BF16)
    K.copy(identb[:], ident, [b_c], [b_identb], eng="dve")
    maskb, b_maskb = K.tile("maskb", [128, 128], BF16)
    K.ts(maskb[:], m_full, 30000.0, -30000.0, ALU.mult, ALU.add, [b_c], [b_maskb])
    onesr, b_onesr = K.tile("onesr", [128, 64])
    K.memset(onesr[:], 1.0, [b_onesr])
    ones64 = cs[0:64, 128:192]
    fs, b_fs = K.tile("fs", [128, 130])
    K.load(fs[:], f2[:, :], b_fs)
    nfb, b_nfb = K.tile("nfb", [128, 1])
    K.ts(nfb[:], hp_[:, 2:3], -1.0, None, ALU.mult, None, [b_hp], [b_nfb])
    K.act(fs[:], fs[:], AF.Exp, [b_fs, b_nfb], [b_fs], bias=nfb[:, 0:1], scale=-1.0)
    K.act(fs[:], fs[:], AF.Ln, [b_fs], [b_fs], bias=1.0)
    K.ts(fs[:], fs[:], -1.0, None, ALU.mult, None, [b_fs], [b_fs])
    K.memset(fs[0:1, 0:NPAD], 0.0, [b_fs], eng="dve")
    K.memset(fs[64:65, 0:NPAD], 0.0, [b_fs], eng="dve")
    one2, b_one2 = K.tile("one2", [128, 130])
    K.memset(one2[:], 1.0, [b_one2])
    fc, b_fc = K.tile("fc", [128, 130])
    K.scan(fc[:], one2[:], fs[:], 0.0, ALU.mult, ALU.add, [b_one2, b_fs], [b_fc])
    pso, b_pso = K.ps()
    K.mm(pso[:, 0:1], m_su, fc[:, 129:130], True, True, [b_c, b_fc], [b_pso])
    off, b_off = K.tile("foff", [128, 1])
    K.copy(off[:], pso[:, 0:1], [b_pso], [b_off], eng="dve")
    K.ts(fc[:], fc[:], off[:, 0:1], None, ALU.add, None, [b_fc, b_off], [b_fc])
    F3, b_F3 = K.tile("F3", [128, 3, 130], BF16)
    NF3, b_NF3 = K.tile("NF3", [128, 3, 130], BF16)
    res_, b_res = K.tile("fres", [128, 130])
    for i in range(3):
        src = fc if i == 0 else res_
        b_src = b_fc if i == 0 else b_res
        K.copy(F3[:, i, :], src[:], [b_src], [b_F3], eng="dve")
        if i < 2:
            K.copy(one2[:], F3[:, i, :], [b_F3], [b_one2], eng="dve")
            K.tt(res_[:], src[:], one2[:], ALU.subtract, [b_src, b_one2], [b_res])
    K.ts(NF3[:], F3[:], -1.0, None, ALU.mult, None, [b_F3], [b_NF3])
    fsc = K.nc.dram_tensor("fsc", [2, 2, 3, LP], BF16, kind="Internal").ap()
    b_fsc = K.P.buf("fsc")
    for h in range(2):
        for i in range(3):
            K.P.dma("sp", fsc[0, h, i, :].rearrange("(p j) -> p j", j=130), F3[h * 64:(h + 1) * 64, i, :], b_F3,
                    reads=[b_F3], writes=[b_fsc])
            K.P.dma("sp", fsc[1, h, i, :].rearrange("(p j) -> p j", j=130), NF3[h * 64:(h + 1) * 64, i, :], b_NF3,
                    reads=[b_NF3], writes=[b_fsc])
    qts = [(n0, min(512, LP - n0)) for n0 in range(0, LP, 512)]
    for h in range(2):
        Qa, b_Qa = K.tmp("Qa", [70, LP], BF16, 1)
        Ka, b_Ka = K.tmp("Ka", [70, LP], BF16, 1)
        K.memset(Qa[64:70, :], 1.0, [b_Qa])
        K.memset(Ka[64:70, :], 1.0, [b_Ka])
        K.P.dma("sp", Qa[64:67, :], fsc[0, h, :, :], b_Qa, reads=[b_fsc], writes=[b_Qa])
        K.P.dma("sp", Ka[67:70, :], fsc[1, h, :, :], b_Ka, reads=[b_fsc], writes=[b_Ka])
        for (src, dstt, b_dst, gcol, sc) in ((qT2, Qa, b_Qa, 0, 0.125), (kT2, Ka, b_Ka, 1, 1.0)):
            gs, b_gs = K.tmp("gs", [64, 1], F32, 2)
            K.ts(gs[:], hp_[0:64, gcol:gcol + 1], sc, None, ALU.mult, None, [b_hp], [b_gs])
            for (n0, nn) in qts:
                xq, b_xq = K.tmp("xq", [64, 512], F32, 2)
                K.load(xq[:, :nn], src[h, :, n0:n0 + nn], b_xq)
                sq, b_sq = K.tmp("sqq", [64, 512], F32, 2)
                K.act(sq[:, :nn], xq[:, :nn], AF.Square, [b_xq], [b_sq])
                ps, b_ps = K.ps()
                K.mm(ps[0:64, :nn], ones64, sq[:, :nn], True, True, [b_c, b_sq], [b_ps])
                rs, b_rs = K.tmp("rsq", [64, 512], F32, 2)
                K.act(rs[:, :nn], ps[0:64, :nn], AF.Sqrt, [b_ps], [b_rs], bias=NORM_EPS, scale=1.0 / 64)
                K.recip(rs[:, :nn], rs[:, :nn], [b_rs], [b_rs])
                K.stt(dstt[0:64, n0:n0 + nn], xq[:, :nn], gs[:, 0:1], rs[:, :nn], ALU.mult, ALU.mult,
                      [b_xq, b_gs, b_rs], [b_dst])
        Va, b_Va = K.tmp("Va", [128, 65, 65], BF16, 1)
        K.memset(Va[:, :, 64:65], 1.0, [b_Va])
        K.memset(Va[0:NPAD, 0, 64:65], 0.0, [b_Va])
        vvw = v2[:, h * 64:(h + 1) * 64].rearrange("(b s) d -> s b d", s=128)
        for b0 in range(0, 65, 13):
            K.load(Va[:, b0:b0 + 13, 0:64], vvw[:, b0:b0 + 13, :], b_Va)
        for m, (q0, W) in enumerate(qts):
            O_ps, b_O = K.tmp_ps("Ops")
            nblk = W // 128
            jlast = 4 * m + nblk - 1
            for J in range(jlast + 1):
                i = J - 4 * m
                S_ps, b_S = K.ps()
                Pt, b_Pt = K.tmp("Pt", [128, 512], BF16, 3)
                kb = Ka[:, J * 128:(J + 1) * 128]
                if i < 0:
                    c0 = 0
                    K.mm(S_ps[:, 0:W], kb, Qa[:, q0:q0 + W], True, True, [b_Ka, b_Qa], [b_S])
                else:
                    c0 = 128 * i
                    K.mm(S_ps[:, c0:c0 + 128], kb, Qa[:, q0 + c0:q0 + c0 + 128], True, False, [b_Ka, b_Qa], [b_S])
                    K.mm(S_ps[:, c0:c0 + 128], identb[:], maskb[:], False, True, [b_identb, b_maskb], [b_S])
                    if c0 + 128 < W:
                        K.mm(S_ps[:, c0 + 128:W], kb, Qa[:, q0 + c0 + 128:q0 + W], True, True, [b_Ka, b_Qa], [b_S])
                K.act(Pt[:, c0:W], S_ps[:, c0:W], AF.Exp, [b_S], [b_Pt])
                K.mm(O_ps[0:65, c0:W], Va[:, J, :], Pt[:, c0:W], J == 0, J == jlast, [b_Va, b_Pt], [b_O])
            Osb, b_Osb = K.tmp("Osb", [65, 512], F32, 2)
            K.copy(Osb[:, :W], O_ps[0:65, :W], [b_O], [b_Osb], eng="dve")
            K.recip(Osb[64:65, :W], Osb[64:65, :W], [b_Osb], [b_Osb])
            bc, b_bc = K.ps()
            K.mm(bc[0:64, :W], onesr[64:65, 0:64], Osb[64:65, :W], True, True, [b_onesr, b_Osb], [b_bc])
            ofs, b_ofs = K.tmp("ofs", [64, 512], F32, 2)
            K.tt(ofs[:, :W], Osb[0:64, :W], bc[0:64, :W], ALU.mult, [b_Osb, b_bc], [b_ofs])
            K.store(ofT[h * 64:(h + 1) * 64, q0:q0 + W], ofs[:, :W], b_ofs)


def emit_C(K, hT, ofT, yt, vr, bs, gtok, gatesT, w_bf, w_br, w_o, ffn_up, ffn_conv, ffn_down, gnw, gnb, gffn, ccst, hout):
    cs, b_c = K.tile("cc", [128, 256])
    K.load(cs[:], ccst[:, :], b_c)
    ident, ones_f = cs[:, 0:128], cs[:, 128:256]
    gw, b_gw = K.tile("gw", [128, 512]); K.load(gw[:], gnw[0:1, :].partition_broadcast(128), b_gw)
    gbb, b_gbb = K.tile("gbb", [128, 512]); K.load(gbb[:], gnb[0:1, :].partition_broadcast(128), b_gbb)
    gf, b_gf = K.tile("gf", [128, 8]); K.load(gf[:], gffn[:, :], b_gf)
    cw, b_cw = K.tile("cw", [128, 3, 44]); K.load(cw[:], ffn_conv[:, :, :], b_cw)
    hm, b_hm = K.tile("hm", [128, 8, TOK])
    R2, b_R2 = K.tile("R2", [128, 8, TOK + 2], BF16)
    wb, b_wb = K.tile("wbig", [128, 8, 1024], BF16)
    K.load(wb[:, 0:4, :], w_bf.rearrange("(kc p) c -> p kc c", p=128), b_wb, eng="pool")
    K.load(wb[:, 4:8, :], w_br.rearrange("(kc p) c -> p kc c", p=128), b_wb, eng="pool")
    ofv = ofT.rearrange("(kc p) t -> p kc t", p=128)
    v3 = lambda ap, n: ap.rearrange("p (a b) -> p a b", a=8)
    for (n0, nn) in NTL:
        R1, b_R1 = K.tmp("R1", [128, 8, 512], BF16, 1)
        K.load(R1[:, 0:4, :nn], ofv[:, :, n0:n0 + nn], b_R1, eng="pool")
        for (t0, nt) in TBL:
            if not (n0 <= t0 < n0 + nn):
                continue
            yb, b_yb = K.tmp("yb", [128, 512], F32, 1); K.load(yb[:nt, :], yt[t0:t0 + nt, :], b_yb)
            vb, b_vb = K.tmp("vb", [128, 512], F32, 1); K.load(vb[:nt, :], vr[t0:t0 + nt, :], b_vb)
            gb_, b_gb_ = K.tmp("gtb", [128, 512], F32, 1); K.load(gb_[:nt, :], gtok[t0:t0 + nt, :], b_gb_)
            bsb, b_bsb = K.tmp("bsb", [128, 8], F32, 2); K.load(bsb[:nt, :], bs[t0:t0 + nt, :], b_bsb)
            st8, b_st8 = K.tmp("st8", [128, 8], F32, 2)
            K.P.op("dve", (lambda o, i: (lambda e: e.tensor_reduce(out=o, in_=i, axis=AX.X, op=ALU.add)))(st8[:nt, :], v3(yb[:nt, :], 8)),
                   [b_yb], [b_st8])
            K.ts(st8[:nt, :], st8[:nt, :], 1.0 / 64, None, ALU.mult, None, [b_st8], [b_st8])
            K.tt(v3(yb[:nt, :], 8), v3(yb[:nt, :], 8), st8[:nt, :].unsqueeze(2).to_broadcast([nt, 8, 64]), ALU.subtract,
                 [b_yb, b_st8], [b_yb])
            sq, b_sq = K.tmp("csq", [128, 512], F32, 1)
            K.tt(sq[:nt, :], yb[:nt, :], yb[:nt, :], ALU.mult, [b_yb], [b_sq], eng="pool")
            v8, b_v8 = K.tmp("v8", [128, 8], F32, 2)
            K.P.op("dve", (lambda o, i: (lambda e: e.tensor_reduce(out=o, in_=i, axis=AX.X, op=ALU.add)))(v8[:nt, :], v3(sq[:nt, :], 8)),
                   [b_sq], [b_v8])
            K.act(v8[:nt, :], v8[:nt, :], AF.Sqrt, [b_v8], [b_v8], bias=GN_EPS, scale=1.0 / 64)
            K.recip(v8[:nt, :], v8[:nt, :], [b_v8], [b_v8])
            K.tt(v3(yb[:nt, :], 8), v3(yb[:nt, :], 8), v8[:nt, :].unsqueeze(2).to_broadcast([nt, 8, 64]), ALU.mult,
                 [b_yb, b_v8], [b_yb])
            K.tt(yb[:nt, :], yb[:nt, :], gw[:nt, :], ALU.mult, [b_yb, b_gw], [b_yb], eng="pool")
            K.tt(yb[:nt, :], yb[:nt, :], gbb[:nt, :], ALU.add, [b_yb, b_gbb], [b_yb], eng="pool")
            K.tt(v3(vb[:nt, :], 8), v3(vb[:nt, :], 8), bsb[:nt, :].unsqueeze(2).to_broadcast([nt, 8, 64]), ALU.mult,
                 [b_vb, b_bsb], [b_vb])
            K.tt(yb[:nt, :], yb[:nt, :], vb[:nt, :], ALU.add, [b_yb, b_vb], [b_yb], eng="pool")
            K.tt(yb[:nt, :], yb[:nt, :], gb_[:nt, :], ALU.mult, [b_yb, b_gb_], [b_yb])
            for c in range(4):
                ps, b_ps = K.ps()
                K.mm(ps[:, :nt], yb[:nt, c * 128:(c + 1) * 128], ident[:nt, :nt], True, True, [b_yb, b_c], [b_ps])
                K.copy(R1[:, 4 + c, t0 - n0:t0 - n0 + nt], ps[:, :nt], [b_ps], [b_R1])
        for oc in range(8):
            osl = slice(oc * 128, (oc + 1) * 128)
            psF, b_psF = K.ps()
            psR, b_psR = K.ps()
            for kc in range(4):
                K.mm(psF[:, :nn], wb[:, kc, osl], R1[:, kc, :nn], kc == 0, kc == 3, [b_wb, b_R1], [b_psF])
            for kc in range(4):
                K.mm(psR[:, :nn], wb[:, 4 + kc, osl], R1[:, 4 + kc, :nn], kc == 0, kc == 3, [b_wb, b_R1], [b_psR])
            g1, b_g1 = K.tmp("g1", [128, 512], F32, 2); K.load(g1[:, :nn], gatesT[oc * 128:(oc + 1) * 128, n0:n0 + nn], b_g1)
            g2, b_g2 = K.tmp("g2", [128, 512], F32, 2); K.load(g2[:, :nn], gatesT[1024 + oc * 128:1024 + (oc + 1) * 128, n0:n0 + nn], b_g2)
            K.tt(g1[:, :nn], g1[:, :nn], psF[:, :nn], ALU.mult, [b_g1, b_psF], [b_g1])
            K.tt(g2[:, :nn], g2[:, :nn], psR[:, :nn], ALU.mult, [b_g2, b_psR], [b_g2])
            K.tt(R2[:, oc, n0:n0 + nn], g1[:, :nn], g2[:, :nn], ALU.add, [b_g1, b_g2], [b_R2], eng="pool")
    K.load(wb[:, :, :], w_o.rearrange("(kc p) c -> p kc c", p=128), b_wb, eng="pool")
    hv = hT.rearrange("(kc p) t -> p kc t", p=128)
    for oc in range(8):
        osl = slice(oc * 128, (oc + 1) * 128)
        for (n0, nn) in NTL:
            ps, b_ps = K.ps()
            for kc in range(8):
                K.mm(ps[:, :nn], wb[:, kc, osl], R2[:, kc, n0:n0 + nn], kc == 0, kc == 7, [b_wb, b_R2], [b_ps])
            ht, b_ht = K.tmp("ht", [128, 512], F32, 2)
            K.load(ht[:, :nn], hv[:, oc, n0:n0 + nn], b_ht)
            K.tt(hm[:, oc, n0:n0 + nn], ht[:, :nn], ps[:, :nn], ALU.add, [b_ht, b_ps], [b_hm])
    K.memset(R2[:, :, 0:2], 0.0, [b_R2])
    for (n0, nn) in NTL:
        ps, b_ps = K.ps()
        for kc in range(8):
            sq, b_sq = K.tmp("csq", [128, 512], F32, 1)
            K.act(sq[:, :nn], hm[:, kc, n0:n0 + nn], AF.Square, [b_hm], [b_sq])
            K.mm(ps[:, :nn], ones_f, sq[:, :nn], kc == 0, kc == 7, [b_sq, b_c], [b_ps])
        rs, b_rs = K.tmp("crs", [128, 512], F32, 1)
        K.act(rs[:, :nn], ps[:, :nn], AF.Sqrt, [b_ps], [b_rs], bias=NORM_EPS, scale=1.0 / 1024)
        K.recip(rs[:, :nn], rs[:, :nn], [b_rs], [b_rs])
        for kc in range(8):
            K.stt(R2[:, kc, 2 + n0:2 + n0 + nn], hm[:, kc, n0:n0 + nn], gf[:, kc:kc + 1], rs[:, :nn],
                  ALU.mult, ALU.mult, [b_hm, b_gf, b_rs], [b_R2])
    uv_ = ffn_up.rearrange("(kc p) c -> p kc c", p=128)
    dv_ = ffn_down.rearrange("(c p) o -> p c o", p=128)
    NT2 = [(0, 18)] + [(18 + 512 * i, 512) for i in range(4)]

    def up_half(col0, cidx, key):
        wt, b_wt = K.tmp("wu" + key, [128, 8, 128], BF16, 2)
        K.load(wt[:], uv_[:, :, col0:col0 + 128], b_wt, eng="pool")
        u, b_u = K.tmp("u" + key, [128, TOK + 2], F32, 1)
        for (n0, nn) in NT2:
            ps, b_ps = K.ps()
            for kc in range(8):
                K.mm(ps[:, :nn], wt[:, kc, :], R2[:, kc, n0:n0 + nn], kc == 0, kc == 7, [b_wt, b_R2], [b_ps])
            K.copy(u[:, n0:n0 + nn], ps[:, :nn], [b_ps], [b_u])
        return u, b_u

    for i in range(22):
        ug, b_ug = up_half(i * 128, i, "g")
        uvv, b_uv = up_half(2816 + i * 128, 22 + i, "v")
        wd, b_wd = K.tmp("wd", [128, 1024], BF16, 2)
        K.load(wd[:], dv_[:, i, :], b_wd, eng="pool")
        for (n0, nn) in NTL:
            cg, b_cg = K.tmp("cg", [128, 512], F32, 2)
            cv, b_cv = K.tmp("cv", [128, 512], F32, 2)
            for (u, b_u, cc, b_cc, ci, eng) in ((ug, b_ug, cg, b_cg, i, "dve"), (uvv, b_uv, cv, b_cv, 22 + i, "dve")):
                K.ts(cc[:, :nn], u[:, 2 + n0:2 + n0 + nn], cw[:, 2, ci:ci + 1], None, ALU.mult, None, [b_u, b_cw], [b_cc], eng=eng)
                K.stt(cc[:, :nn], u[:, 1 + n0:1 + n0 + nn], cw[:, 1, ci:ci + 1], cc[:, :nn], ALU.mult, ALU.add, [b_u, b_cw, b_cc], [b_cc], eng=eng)
                K.stt(cc[:, :nn], u[:, n0:n0 + nn], cw[:, 0, ci:ci + 1], cc[:, :nn], ALU.mult, ALU.add, [b_u, b_cw, b_cc], [b_cc], eng=eng)
            K.act(cg[:, :nn], cg[:, :nn], AF.Silu, [b_cg], [b_cg])
            at_, b_at = K.tmp("actT", [128, 512], BF16, 2)
            K.tt(at_[:, :nn], cg[:, :nn], cv[:, :nn], ALU.mult, [b_cg, b_cv], [b_at])
            for oc in range(8):
                ps, b_ps = K.ps()
                K.mm(ps[:, :nn], wd[:, oc * 128:(oc + 1) * 128], at_[:, :nn], True, True, [b_wd, b_at], [b_ps])
                K.tt(hm[:, oc, n0:n0 + nn], hm[:, oc, n0:n0 + nn], ps[:, :nn], ALU.add, [b_hm, b_ps], [b_hm])
    ho = hout.rearrange("(kc p) t -> p kc t", p=128)
    for kc in range(8):
        K.store(ho[:, kc, :], hm[:, kc, :], b_hm)


def _build_A():
    nc = bass.Bass("TRN2", target_bir_lowering=False)
    K = KB(nc, n_ps=7)
    ins = dict(hT=[1024, TOK], w_in=[1024, 5384], gmix=[128, 8], gbias=[128, 16], mu=[128, 14], pr=[128, 5, 4],
               w_up=[64, 512], a_up=[64, 512], g_up=[128, 512], muv=[1, 512], cst=[128, 258])
    I = {k: K.dram_in(k, v) for k, v in ins.items()}
    O = dict(qkT=K.dram_out("qkT", [1024, TOK]), vtok=K.dram_out("vtok", [TOK, 512], BF16), fT=K.dram_out("fT", [8, TOK]),
             rw5=K.dram_out("rw5", [5, 512, TOK]), vr=K.dram_out("vr", [TOK, 512]), bs=K.dram_out("bs", [TOK, 8]),
             gtok=K.dram_out("gtok", [TOK, 512]), gatesT=K.dram_out("gatesT", [2048, TOK]))
    emit_A(K, **I, **O)
    return K.finish()


def _build_H():
    nc = bass.Bass("TRN2", target_bir_lowering=False)
    K = KB(nc, n_ps=6)
    I = dict(qT2=K.dram_in("qT2", [2, 64, LP]), kT2=K.dram_in("kT2", [2, 64, LP]), v2=K.dram_in("v2", [LP, 128], BF16),
             f2=K.dram_in("f2", [128, 130]), rw5h=K.dram_in("rw5h", [5, 128, LP]), vrh=K.dram_in("vrh", [LP, 128]),
             hpar=K.dram_in("hpar", [128, 4]), hcst=K.dram_in("hcst", [128, 768]))
    O = dict(ofT=K.dram_out("ofT", [128, LP]), yout=K.dram_out("yout", [LP, 128]))
    emit_H(K, **I, **O)
    return K.finish()


def _build_C():
    nc = bass.Bass("TRN2", target_bir_lowering=False)
    K = KB(nc, n_ps=8)
    ins = dict(hT=[1024, TOK], ofT=[512, TOK], yt=[TOK, 512], vr=[TOK, 512], bs=[TOK, 8], gtok=[TOK, 512], gatesT=[2048, TOK],
               w_bf=[512, 1024], w_br=[512, 1024], w_o=[1024, 1024], ffn_up=[1024, 5632], ffn_conv=[128, 3, 44],
               ffn_down=[2816, 1024], gnw=[1, 512], gnb=[1, 512], gffn=[128, 8], ccst=[128, 256])
    I = {k: K.dram_in(k, v) for k, v in ins.items()}
    emit_C(K, **I, hout=K.dram_out("hout", [1024, TOK]))
    return K.finish()


def _consts():
    cst = np.zeros((128, 258), np.float32)
    cst[:, :128] = 1; cst[:64, 128:192] = 1; cst[64:, 192:256] = 1; cst[:64, 256] = 1; cst[64:, 257] = 1
    hc = np.zeros((128, 768), np.float32)
    hc[:, 0:128] = np.eye(128)
    bd = np.zeros((128, 128), np.float32); bd[:64, :64] = 1; bd[64:, 64:] = 1
    s_ = np.arange(128)[:, None]; t_ = np.arange(128)[None, :]
    hc[:, 128:256] = bd; hc[:, 256:384] = bd * (s_ < t_); hc[:, 384:512] = bd * (s_ > t_); hc[:, 512:640] = bd * (s_ <= t_)
    hc[:, 640:768] = (s_ <= t_)
    cc = np.zeros((128, 256), np.float32); cc[:, :128] = np.eye(128); cc[:, 128:] = 1
    return cst, hc, cc


def kernel(x, meta_tokens, norm_mix, norm_ffn, w_in, fox_q_norm, fox_k_norm, fox_f_bias,
           rwkv_shift_mu, rwkv_w0, rwkv_w_up, rwkv_a0, rwkv_a_up, rwkv_g_up, rwkv_k_k,
           rwkv_k_a, rwkv_r_k, rwkv_gn_w, rwkv_gn_b, w_branch_fox, w_branch_rwkv,
           gate_bias, w_out, ffn_up, ffn_conv, ffn_down):
    f32 = np.float32
    A = lambda a: np.ascontiguousarray(np.asarray(a, dtype=f32))
    x = A(x); meta = A(meta_tokens)
    B, S, D = x.shape
    ncore = 8
    cst, hc, cc = _consts()
    pm = lambda v, n: np.ascontiguousarray(np.asarray(v, f32).reshape(n, 128).T)
    ncA, ncH, ncC = _build_A(), _build_H(), _build_C()
    ids = list(range(ncore))
    hT = []
    for c in range(ncore):
        b, j = divmod(c, 4)
        if j == 0:
            tok = np.concatenate([meta, x[b, 0:2048]], 0)
        else:
            tok = x[b, 2048 * j - 16:2048 * j + 2048]
        hT.append(np.ascontiguousarray(tok.T))
    depth = np.asarray(w_in).shape[0]
    for L in range(depth):
        pr = np.ascontiguousarray(np.stack([pm(np.asarray(t)[L].reshape(-1), 4) for t in
                                            (rwkv_w0, rwkv_a0, rwkv_k_k, rwkv_k_a, rwkv_r_k)], 1))
        mu_l = np.asarray(rwkv_shift_mu, f32)[L]
        shared = dict(w_in=A(np.asarray(w_in)[L]), gmix=pm(np.asarray(norm_mix)[L], 8), gbias=pm(np.asarray(gate_bias)[L], 16),
                      mu=pm(mu_l, 14), pr=pr, w_up=A(np.asarray(rwkv_w_up)[L]), a_up=A(np.asarray(rwkv_a_up)[L]),
                      g_up=A(np.asarray(rwkv_g_up)[L]), muv=np.ascontiguousarray(mu_l[None, 1024:1536]), cst=cst)
        ra = run_bass_kernel_spmd(ncA, [dict(hT=hT[c], **shared) for c in ids], core_ids=ids).results
        him = []
        for c in ids:
            b, p = divmod(c, 4)
            def cat(fn):
                return np.concatenate([fn(ra[b * 4 + j]) if j == 0 else fn(ra[b * 4 + j], 16) for j in range(4)], -1)
            hs_ = slice(p * 128, (p + 1) * 128)
            def fm(name, rows):
                parts = [np.asarray(ra[b * 4 + j][name])[..., rows, (0 if j == 0 else 16):] for j in range(4)]
                full = np.concatenate(parts, -1)
                out = np.zeros(full.shape[:-1] + (LP,), full.dtype); out[..., NPAD:] = full
                return out
            def tm(name, cols):
                parts = [np.asarray(ra[b * 4 + j][name])[(0 if j == 0 else 16):, cols] for j in range(4)]
                full = np.concatenate(parts, 0)
                out = np.zeros((LP, full.shape[1]), full.dtype); out[NPAD:] = full
                return out
            hpar = np.zeros((128, 4), f32)
            hpar[:64, 0] = np.asarray(fox_q_norm, f32)[L]; hpar[:64, 1] = np.asarray(fox_k_norm, f32)[L]
            hpar[:64, 2] = np.asarray(fox_f_bias, f32)[L][2 * p]; hpar[64:, 2] = np.asarray(fox_f_bias, f32)[L][2 * p + 1]
            him.append(dict(qT2=np.ascontiguousarray(fm("qkT", hs_).reshape(2, 64, LP)),
                            kT2=np.ascontiguousarray(fm("qkT", slice(512 + p * 128, 512 + (p + 1) * 128)).reshape(2, 64, LP)),
                            v2=np.ascontiguousarray(tm("vtok", hs_)),
                            f2=np.ascontiguousarray(fm("fT", slice(2 * p, 2 * p + 2)).reshape(128, 130)),
                            rw5h=np.ascontiguousarray(fm("rw5", hs_)), vrh=np.ascontiguousarray(tm("vr", hs_)),
                            hpar=hpar, hcst=hc))
        rh = run_bass_kernel_spmd(ncH, him, core_ids=ids).results
        sharedC = dict(w_bf=A(np.asarray(w_branch_fox)[L]), w_br=A(np.asarray(w_branch_rwkv)[L]), w_o=A(np.asarray(w_out)[L]),
                       ffn_up=A(np.asarray(ffn_up)[L]),
                       ffn_conv=np.ascontiguousarray(np.asarray(ffn_conv, f32)[L].reshape(3, 44, 128).transpose(2, 0, 1)),
                       ffn_down=A(np.asarray(ffn_down)[L]), gnw=A(np.asarray(rwkv_gn_w)[L][None]), gnb=A(np.asarray(rwkv_gn_b)[L][None]),
                       gffn=pm(np.asarray(norm_ffn)[L], 8), ccst=cc)
        cim = []
        for c in ids:
            b, j = divmod(c, 4)
            of_full = np.concatenate([np.asarray(rh[b * 4 + p]["ofT"])[:, NPAD:] for p in range(4)], 0)
            y_full = np.concatenate([np.asarray(rh[b * 4 + p]["yout"])[NPAD:] for p in range(4)], 1)
            t0 = 2048 * j
            cim.append(dict(hT=hT[c], ofT=np.ascontiguousarray(of_full[:, t0:t0 + TOK]), yt=np.ascontiguousarray(y_full[t0:t0 + TOK]),
                            vr=np.asarray(ra[c]["vr"]), bs=np.asarray(ra[c]["bs"]), gtok=np.asarray(ra[c]["gtok"]),
                            gatesT=np.asarray(ra[c]["gatesT"]), **sharedC))
        rc = run_bass_kernel_spmd(ncC, cim, core_ids=ids).results
        hT = [np.asarray(rc[c]["hout"]) for c in ids]
    out = np.zeros((B, S, D), f32)
    for c in ids:
        b, j = divmod(c, 4)
        out[b, 2048 * j:2048 * j + 2048] = hT[c][:, 16:].T
    return out
```
